# Optimizing a Trainium2 kernel written in Bass

```python
import math, functools
import jax, jax.numpy as jnp
from jax import lax
import numpy as np

D_MODEL = 2048
BATCH = 1
SEQ = 8192
DEPTH = 4

GRID_W = 64
CTX_LEN = 256
N_MIXERS = 2
S5_GROUP = 16
S5_GROUPS = D_MODEL // S5_GROUP
S5_STATE = 64
S5_DT_MIN = 0.001
S5_DT_MAX = 0.1
HG_HEADS = 16
HG_DK = D_MODEL // HG_HEADS
HG_DV = D_MODEL // HG_HEADS
HG_CHUNK = 32
MLP_HIDDEN = 4 * D_MODEL
NORM_EPS = 1e-6

kernel_name = "hybrid_s5_hgrn2_prefix_dit"


def rmsnorm(x, g):
    xf = x.astype(jnp.float32)
    y = xf * lax.rsqrt(jnp.mean(xf * xf, axis=-1, keepdims=True) + NORM_EPS)
    return (y * g.astype(jnp.float32)).astype(x.dtype)


def adaln(cond, w, b):
    return jnp.split(jax.nn.silu(cond) @ w + b, 6, axis=-1)


def modulate(h, shift, scale):
    return h * (1 + scale) + shift


def raster_to_colmajor(h):
    b, n, d = h.shape
    rows = n // GRID_W
    return h.reshape(b, rows, GRID_W, d).transpose(0, 2, 1, 3).reshape(b, n, d)


def colmajor_to_raster(h):
    b, n, d = h.shape
    rows = n // GRID_W
    return h.reshape(b, GRID_W, rows, d).transpose(0, 2, 1, 3).reshape(b, n, d)


def sq_relu_mlp(h, w1, w2):
    return jnp.square(jax.nn.relu(h @ w1)) @ w2


def s5_discretise(a_re, a_im, log_dt, b_re, b_im):
    lam = lax.complex(a_re.astype(jnp.float32), a_im.astype(jnp.float32))
    dt = jnp.exp(log_dt.astype(jnp.float32))[:, None]
    lam_bar = jnp.exp(lam * dt)
    b = lax.complex(b_re.astype(jnp.float32), b_im.astype(jnp.float32))
    b_bar = ((lam_bar - 1) / lam)[..., None] * b
    return lam_bar, b_bar


def _lin_rec_combine(e1, e2):
    a1, b1 = e1
    a2, b2 = e2
    return a1 * a2, a2 * b1 + b2


def s5_scan(u, lam_bar, b_bar, s0, reverse):
    bu = jnp.einsum('btgh,gph->btgp', u, b_bar)
    first = -1 if reverse else 0
    bu = bu.at[:, first].add(lam_bar * s0)
    a = jnp.broadcast_to(lam_bar, bu.shape)
    _, states = lax.associative_scan(_lin_rec_combine, (a, bu), axis=1, reverse=reverse)
    return states


def s5_mixer(h_ctx, h_lat, a_re, a_im, log_dt, b_re, b_im, c_re, c_im, d_skip, w_glu, ctx_out):
    def to_groups(h):
        return h.astype(jnp.float32).reshape(h.shape[0], h.shape[1], S5_GROUPS, S5_GROUP)

    u_ctx, u_lat = to_groups(h_ctx), to_groups(h_lat)
    uc_ctx, uc_lat = u_ctx.astype(jnp.complex64), u_lat.astype(jnp.complex64)
    d = d_skip.astype(jnp.float32).reshape(S5_GROUPS, S5_GROUP)
    y_lat = u_lat * d
    y_ctx = u_ctx * d if ctx_out else None
    s0 = jnp.zeros((h_lat.shape[0], S5_GROUPS, S5_STATE), jnp.complex64)
    for direction in range(2):
        reverse = direction == 1
        lam_bar, b_bar = s5_discretise(a_re[direction], a_im[direction], log_dt[direction],
                                       b_re[direction], b_im[direction])
        c_mat = lax.complex(c_re[direction].astype(jnp.float32), c_im[direction].astype(jnp.float32))
        st_ctx = s5_scan(uc_ctx, lam_bar, b_bar, s0, reverse)
        s_ctx_final = st_ctx[:, 0] if reverse else st_ctx[:, -1]
        st_lat = s5_scan(uc_lat, lam_bar, b_bar, s_ctx_final, reverse)
        y_lat = y_lat + jnp.einsum('btgp,ghp->btgh', st_lat, c_mat).real
        if ctx_out:
            y_ctx = y_ctx + jnp.einsum('btgp,ghp->btgh', st_ctx, c_mat).real

    def glu_out(y, h):
        z = jax.nn.gelu(y.reshape(h.shape).astype(h.dtype))
        a, g = jnp.split(z @ w_glu, 2, axis=-1)
        return a * jax.nn.sigmoid(g)

    return (glu_out(y_ctx, h_ctx) if ctx_out else None), glu_out(y_lat, h_lat)


def hgrn_project(h, w_in, with_query):
    def heads(t):
        return t.astype(jnp.float32).reshape(t.shape[0], t.shape[1], HG_HEADS, -1)
    if with_query:
        q, i, zf, zb, g = jnp.split(h @ w_in, 5, axis=-1)
        return heads(jax.nn.silu(q)), heads(i), heads(zf), heads(zb), g
    i, zf, zb = jnp.split(h @ w_in[:, D_MODEL:4 * D_MODEL], 3, axis=-1)
    return None, heads(i), heads(zf), heads(zb), None


def forget_gate(logit, lb):
    lb = lb.astype(jnp.float32).reshape(HG_HEADS, HG_DK)
    log_f = jnp.logaddexp(jax.nn.log_sigmoid(logit), jnp.log(lb) + jax.nn.log_sigmoid(-logit))
    k = (1 - lb) * jax.nn.sigmoid(-logit)
    return log_f, k


def chunked_gated_recurrence(q, k, v, logf, s0):
    b_, t, h, kd = k.shape
    n = t // HG_CHUNK

    def chunks(z):
        return z.reshape(b_, n, HG_CHUNK, h, z.shape[-1])

    k, v, logf = chunks(k), chunks(v), chunks(logf)
    cum = jnp.cumsum(logf, axis=2)
    cum_last = cum[:, :, -1]
    k_end = k * jnp.exp(cum_last[:, :, None] - cum)
    u = jnp.einsum('bnchk,bnchv->bnhkv', k_end, v)

    def step(s, inp):
        decay, inc = inp
        return decay[..., None] * s + inc, s

    s_fin, s_in = lax.scan(step, s0, (jnp.moveaxis(jnp.exp(cum_last), 1, 0), jnp.moveaxis(u, 1, 0)))
    if q is None:
        return None, s_fin
    s_in = jnp.moveaxis(s_in, 0, 1)
    q = chunks(q)
    qb = q * jnp.exp(cum)
    kb = k * jnp.exp(-cum)
    scores = jnp.einsum('bnchk,bnshk->bnhcs', qb, kb)
    mask = jnp.tril(jnp.ones((HG_CHUNK, HG_CHUNK), dtype=bool))
    scores = jnp.where(mask, scores, 0.0)
    o = jnp.einsum('bnhcs,bnshv->bnchv', scores, v) + jnp.einsum('bnchk,bnhkv->bnchv', qb, s_in)
    return o.reshape(b_, t, h, v.shape[-1]), s_fin


def hgrn_readout(o, g, o_norm, w_out):
    b_, t, h, dv = o.shape
    o = o * lax.rsqrt(jnp.mean(o * o, axis=-1, keepdims=True) + NORM_EPS)
    o = o.reshape(b_, t, h * dv) * o_norm.astype(jnp.float32)
    return (o.astype(g.dtype) * jax.nn.silu(g)) @ w_out


def hgrn_mixer(h_ctx, h_lat, w_in, lb_fwd, lb_bwd, o_norm, w_out, ctx_out):
    q_l, v_l, zf_l, zb_l, g_l = hgrn_project(h_lat, w_in, True)
    q_c, v_c, zf_c, zb_c, g_c = hgrn_project(h_ctx, w_in, ctx_out)
    s0 = jnp.zeros((h_lat.shape[0], HG_HEADS, HG_DK, HG_DV), jnp.float32)
    flip = functools.partial(jnp.flip, axis=1)
    o_lat, o_ctx = [], []
    for z_c, z_l, lb, reverse in ((zf_c, zf_l, lb_fwd, False), (zb_c, zb_l, lb_bwd, True)):
        orient = flip if reverse else (lambda z: z)
        logf_c, k_c = forget_gate(z_c, lb)
        logf_l, k_l = forget_gate(z_l, lb)
        oc, s_ctx = chunked_gated_recurrence(orient(q_c) if ctx_out else None, orient(k_c), orient(v_c),
                                             orient(logf_c), s0)
        ol, _ = chunked_gated_recurrence(orient(q_l), orient(k_l), orient(v_l), orient(logf_l), s_ctx)
        o_lat.append(orient(ol))
        if ctx_out:
            o_ctx.append(orient(oc))
    y_lat = hgrn_readout(o_lat[0] + o_lat[1], g_l, o_norm, w_out)
    y_ctx = hgrn_readout(o_ctx[0] + o_ctx[1], g_c, o_norm, w_out) if ctx_out else None
    return y_ctx, y_lat


def setup_inputs(seed: int = 0) -> dict:
    key = jax.random.key(seed)
    ks = iter(jax.random.split(key, 32))
    d = D_MODEL
    n_s5 = (DEPTH + N_MIXERS - 1) // N_MIXERS
    n_hg = DEPTH // N_MIXERS
    f32 = jnp.float32

    def nrm(shape, scale):
        return scale * jax.random.normal(next(ks), shape, f32)

    def gain(shape):
        return 1.0 + 0.05 * jax.random.normal(next(ks), shape, f32)

    return {
        "x": nrm((BATCH, SEQ, d), 1.0),
        "c": nrm((BATCH, d), 1.0),
        "ctx": nrm((BATCH, CTX_LEN, d), 1.0),
        "c_ctx": nrm((d,), 1.0),
        "w_mod": nrm((DEPTH, d, 6 * d), 0.5 * d ** -0.5),
        "b_mod": nrm((DEPTH, 6 * d), 0.02),
        "norm_mix_pre": gain((DEPTH, d)),
        "norm_mix_post": gain((DEPTH, d)),
        "norm_mlp_pre": gain((DEPTH, d)),
        "norm_mlp_post": gain((DEPTH, d)),
        "w_mlp_in": nrm((DEPTH, d, MLP_HIDDEN), d ** -0.5),
        "w_mlp_out": nrm((DEPTH, MLP_HIDDEN, d), MLP_HIDDEN ** -0.5),
        "s5_a_re": -0.5 + nrm((n_s5, 2, S5_GROUPS, S5_STATE), 0.01),
        "s5_a_im": math.pi * jnp.arange(S5_STATE, dtype=f32) + nrm((n_s5, 2, S5_GROUPS, S5_STATE), 0.01),
        "s5_log_dt": jax.random.uniform(next(ks), (n_s5, 2, S5_GROUPS), f32,
                                        math.log(S5_DT_MIN), math.log(S5_DT_MAX)),
        "s5_b_re": nrm((n_s5, 2, S5_GROUPS, S5_STATE, S5_GROUP), (2 * S5_GROUP) ** -0.5),
        "s5_b_im": nrm((n_s5, 2, S5_GROUPS, S5_STATE, S5_GROUP), (2 * S5_GROUP) ** -0.5),
        "s5_c_re": nrm((n_s5, 2, S5_GROUPS, S5_GROUP, S5_STATE), S5_STATE ** -0.5),
        "s5_c_im": nrm((n_s5, 2, S5_GROUPS, S5_GROUP, S5_STATE), S5_STATE ** -0.5),
        "s5_d": nrm((n_s5, d), 1.0),
        "s5_w_glu": nrm((n_s5, d, 2 * d), d ** -0.5),
        "hg_w_in": nrm((n_hg, d, 5 * d), d ** -0.5),
        "hg_lb_raw": nrm((2, n_hg, HG_HEADS * HG_DK), 0.1),
        "hg_o_norm": gain((n_hg, d)),
        "hg_w_out": nrm((n_hg, d, d), d ** -0.5),
    }


def reference(x, c, ctx, c_ctx, w_mod, b_mod, norm_mix_pre, norm_mix_post, norm_mlp_pre, norm_mlp_post,
              w_mlp_in, w_mlp_out, s5_a_re, s5_a_im, s5_log_dt, s5_b_re, s5_b_im, s5_c_re, s5_c_im, s5_d,
              s5_w_glu, hg_w_in, hg_lb_raw, hg_o_norm, hg_w_out):
    lb_w = jax.nn.softmax(hg_lb_raw.astype(jnp.float32), axis=1)
    lb = jnp.cumsum(lb_w, axis=1) - lb_w[:, :1]
    x_lat, x_ctx = x, ctx
    for layer in range(DEPTH):
        ctx_live = layer < DEPTH - 1
        j = layer // N_MIXERS
        sh1, sc1, gt1, sh2, sc2, gt2 = adaln(c, w_mod[layer], b_mod[layer])
        csh1, csc1, cgt1, csh2, csc2, cgt2 = adaln(c_ctx, w_mod[layer], b_mod[layer])
        h_lat = modulate(rmsnorm(x_lat, norm_mix_pre[layer]), sh1[:, None], sc1[:, None])
        h_ctx = modulate(rmsnorm(x_ctx, norm_mix_pre[layer]), csh1, csc1)
        if layer % N_MIXERS == 0:
            y_ctx, y_lat = s5_mixer(h_ctx, h_lat, s5_a_re[j], s5_a_im[j], s5_log_dt[j], s5_b_re[j], s5_b_im[j],
                                    s5_c_re[j], s5_c_im[j], s5_d[j], s5_w_glu[j], ctx_live)
        else:
            y_ctx, y_lat = hgrn_mixer(h_ctx, raster_to_colmajor(h_lat), hg_w_in[j], lb[0, j], lb[1, j],
                                      hg_o_norm[j], hg_w_out[j], ctx_live)
            y_lat = colmajor_to_raster(y_lat)
        x_lat = x_lat + gt1[:, None] * rmsnorm(y_lat, norm_mix_post[layer])
        h_lat = modulate(rmsnorm(x_lat, norm_mlp_pre[layer]), sh2[:, None], sc2[:, None])
        x_lat = x_lat + gt2[:, None] * rmsnorm(sq_relu_mlp(h_lat, w_mlp_in[layer], w_mlp_out[layer]),
                                               norm_mlp_post[layer])
        if ctx_live:
            x_ctx = x_ctx + cgt1 * rmsnorm(y_ctx, norm_mix_post[layer])
            h_ctx = modulate(rmsnorm(x_ctx, norm_mlp_pre[layer]), csh2, csc2)
            x_ctx = x_ctx + cgt2 * rmsnorm(sq_relu_mlp(h_ctx, w_mlp_in[layer], w_mlp_out[layer]),
                                           norm_mlp_post[layer])
    return x_lat
```

```python
import contextlib
import numpy as np
import concourse.bass as bass
import concourse.mybir as mybir
from concourse.bass_utils import run_bass_kernel_spmd

F32 = mybir.dt.float32
BF16 = mybir.dt.bfloat16
AF = mybir.ActivationFunctionType
ALU = mybir.AluOpType

D = 2048
NC = 8
SEQ = 8192
CTXL = 256
HID = 8192
TOK = 1056
NTOT = SEQ + CTXL
EPS = 1e-6
TWO_PI = 6.283185307179586
ENGS = ("pe", "act", "dve", "pool", "sp")
NSLOT = 12


class Prog:
    def __init__(self, nc):
        self.nc = nc
        self.ops = {e: [] for e in ENGS}
        self.cnt = {e: 0 for e in ENGS}
        self.lastw = {}
        self.readers = {}
        self.known = {e: {} for e in ENGS}
        self.dma_n = {e: 0 for e in ENGS}
        self.slot_val = {}
        self.out_tokens = []

    def _deps(self, reads, writes):
        waits = {}

        def add(s, v):
            if waits.get(s, 0) < v:
                waits[s] = v

        for k in reads:
            t = self.lastw.get(k)
            if t:
                add(*t)
        for k in writes:
            t = self.lastw.get(k)
            if t:
                add(*t)
            for s, v in self.readers.get(k, {}).items():
                add(s, v)
        return waits

    def _commit(self, eng, fn, waits, tok, reads, writes, inc):
        kn = self.known[eng]
        wl = []
        for s, v in waits.items():
            if kn.get(s, 0) < v:
                kn[s] = v
                wl.append((s, v))
        for k in writes:
            self.lastw[k] = tok
            self.readers[k] = {}
        for k in reads:
            if k in writes:
                continue
            r = self.readers.setdefault(k, {})
            if r.get(tok[0], 0) < tok[1]:
                r[tok[0]] = tok[1]
        self.ops[eng].append((fn, wl, tok[0], inc))

    def op(self, eng, fn, reads=(), writes=()):
        reads, writes = tuple(reads), tuple(writes)
        waits = self._deps(reads, writes)
        self.cnt[eng] += 1
        tok = (("c", eng), self.cnt[eng])
        self._commit(eng, fn, waits, tok, reads, writes, 1)
        return tok

    def dma(self, eng, fn, reads=(), writes=(), out=False):
        reads, writes = tuple(reads), tuple(writes)
        waits = self._deps(reads, writes)
        slot = self.dma_n[eng] % NSLOT
        self.dma_n[eng] += 1
        s = ("d", eng, slot)
        prev = self.slot_val.get(s, 0)
        if prev and waits.get(s, 0) < prev:
            waits[s] = prev
        self.slot_val[s] = prev + 16
        tok = (s, prev + 16)
        self._commit(eng, fn, waits, tok, reads, writes, 16)
        if out:
            self.out_tokens.append(tok)
        return tok

    def emit(self):
        nc = self.nc
        sem_ids = set()
        for e in ENGS:
            for fn, wl, s, inc in self.ops[e]:
                sem_ids.add(s)
        sem_ids = sorted(sem_ids, key=str)
        fin = {}
        for s, v in self.out_tokens:
            fin[s] = max(fin.get(s, 0), v)
        with contextlib.ExitStack() as st:
            sems = {}
            for i, s in enumerate(sem_ids):
                sems[s] = st.enter_context(nc.semaphore("s%d" % i))
            block = st.enter_context(nc.Block())
            handles = {"pe": block.tensor, "act": block.scalar, "dve": block.vector,
                       "pool": block.gpsimd, "sp": block.sync}

            def mk(e):
                oplist = self.ops[e]

                def body(eng):
                    for fn, wl, s, inc in oplist:
                        for ws, v in wl:
                            eng.wait_ge(sems[ws], v)
                        fn(eng).then_inc(sems[s], inc)
                    if e == "sp":
                        for ws, v in fin.items():
                            eng.wait_ge(sems[ws], v)
                return body

            for e in ENGS:
                if self.ops[e] or e == "sp":
                    handles[e](mk(e))


def f_group(fs):
    def g(e):
        inst = None
        for f in fs:
            inst = f(e)
        return inst
    return g


def f_mm(out, lhsT, rhs, start=True, stop=True):
    return lambda e: e.matmul(out, lhsT=lhsT, rhs=rhs, start=start, stop=stop)


def f_tr(out, in_, ident):
    return lambda e: e.transpose(out=out, in_=in_, identity=ident)


def f_act(out, in_, func, bias=None, scale=None, accum_out=None):
    kw = {}
    if bias is not None:
        kw["bias"] = bias
    if scale is not None:
        kw["scale"] = scale
    if accum_out is not None:
        kw["accum_out"] = accum_out
    return lambda e: e.activation(out=out, in_=in_, func=func, **kw)


def f_tt(out, in0, in1, op):
    return lambda e: e.tensor_tensor(out=out, in0=in0, in1=in1, op=op)


def f_ts(out, in0, s1, s2=None, op0=ALU.mult, op1=None):
    if op1 is None:
        return lambda e: e.tensor_scalar(out=out, in0=in0, scalar1=s1, scalar2=None, op0=op0)
    return lambda e: e.tensor_scalar(out=out, in0=in0, scalar1=s1, scalar2=s2, op0=op0, op1=op1)


def f_stt(out, in0, scalar, in1, op0, op1):
    return lambda e: e.scalar_tensor_tensor(out=out, in0=in0, scalar=scalar, in1=in1, op0=op0, op1=op1)


def f_copy(out, in_):
    return lambda e: e.tensor_copy(out=out, in_=in_)


def f_dma(out, in_):
    return lambda e: e.dma_start(out=out, in_=in_)


def f_memset(ap, val):
    return lambda e: e.memset(ap, val)


def f_recip(out, in_):
    return lambda e: e.reciprocal(out=out, in_=in_)


def f_scan(out, d0, d1, init, op0, op1):
    return lambda e: e.tensor_tensor_scan(out=out, data0=d0, data1=d1, initial=init, op0=op0, op1=op1)


class KB:
    def __init__(self):
        self.nc = bass.Bass("TRN2", target_bir_lowering=False)
        self.P = Prog(self.nc)
        self.st = contextlib.ExitStack()
        self.ps = []
        self.uid = 0

    def din(self, name, shape, dt=F32):
        return self.nc.dram_tensor(name, list(shape), dt, kind="ExternalInput")

    def dout(self, name, shape, dt=F32):
        return self.nc.dram_tensor(name, list(shape), dt, kind="ExternalOutput")

    def dscr(self, name, shape, dt=F32):
        return self.nc.dram_tensor(name, list(shape), dt)

    def sb(self, name, shape, dt=F32):
        return self.st.enter_context(self.nc.sbuf_tensor(name, list(shape), dt))

    def psum(self, n=8, dt=F32, cols=512):
        for i in range(n):
            self.ps.append(self.st.enter_context(self.nc.psum_tensor("ps%d" % len(self.ps), [128, cols], dt)))

    def finish(self):
        self.P.emit()
        self.st.close()
        return self.nc


def bcast_row(dram_t, off, n, parts):
    return bass.AP(dram_t, off, [[0, parts], [1, n]])


def build_M():
    kb = KB()
    P = kb.P
    condT = kb.din("condT", [128, 16, 2])
    wm = kb.din("wm", [4, D, 1536])
    bm = kb.din("bm", [4, 1536])
    mod = kb.dout("mod", [4, 2, 1536])
    ct = kb.sb("ct", [128, 16, 2])
    sT = kb.sb("sT", [128, 16, 2])
    bias = kb.sb("bias", [2, 4 * 1536])
    wbuf = [kb.sb("wb%d" % i, [128, 16, 512]) for i in range(2)]
    res = [kb.sb("res%d" % i, [2, 512]) for i in range(2)]
    kb.psum(2)
    P.dma("sp", f_dma(ct[:], condT.ap()), writes=["ct"])
    P.dma("sp", f_dma(bias[:], bcast_row(bm, 0, 4 * 1536, 2)), writes=["bias"])
    P.op("act", f_act(sT[:], ct[:], AF.Silu), reads=["ct"], writes=["sT"])
    it = 0
    for l in range(4):
        wv = wm.ap()[l].rearrange("(kc p) n -> p kc n", p=128)
        for nb in range(3):
            b = it % 2
            P.dma("sp" if it % 2 == 0 else "act", f_dma(wbuf[b][:], wv[:, :, nb * 512:(nb + 1) * 512]),
                  writes=[("wb", b)])
            mms = [f_mm(kb.ps[b][0:2, :], sT[:, kc, :], wbuf[b][:, kc, :], kc == 0, kc == 15) for kc in range(16)]
            P.op("pe", f_group(mms), reads=["sT", ("wb", b)], writes=[("ps", b)])
            o = l * 1536 + nb * 512
            P.op("dve", f_tt(res[b][:], kb.ps[b][0:2, :], bias[:, o:o + 512], ALU.add),
                 reads=[("ps", b), "bias"], writes=[("res", b)])
            P.dma("sp", f_dma(mod.ap()[l, :, nb * 512:(nb + 1) * 512], res[b][:]), reads=[("res", b)], out=True)
            it += 1
    return kb.finish()


class Stat:
    def __init__(self, kb, n=4):
        self.t = kb.sb("stat", [128, 4 * n])
        self.n = n
        self.i = 0

    def next(self):
        i = self.i % self.n
        self.i += 1
        c = 4 * i
        return [(self.t[:, c + j:c + j + 1], ("stat", i, j)) for j in range(4)]


def emit_rstd(P, stat, src, skey, junk, jkey, r, width=D):
    (ssq, k0), (ms, k1), (sd, k2), (rs, k3) = stat.next()
    P.op("act", f_act(junk[:r], src[:r], AF.Square, accum_out=ssq[:r]), reads=[skey], writes=[jkey, k0])
    P.op("dve", f_ts(ms[:r], ssq[:r], 1.0 / width, EPS, ALU.mult, ALU.add), reads=[k0], writes=[k1])
    P.op("act", f_act(sd[:r], ms[:r], AF.Sqrt), reads=[k1], writes=[k2])
    P.op("dve", f_recip(rs[:r], sd[:r]), reads=[k2], writes=[k3])
    return rs, k3


def emit_normmod(P, stat, xt, xkey, ot, okey, A, akey, sh, shkey, r):
    rs, rk = emit_rstd(P, stat, xt, xkey, ot, okey, r)
    P.op("dve", f_stt(ot[:r], xt[:r], rs[:r], A[:r], ALU.mult, ALU.mult), reads=[xkey, rk, akey], writes=[okey])
    P.op("pool", f_tt(ot[:r], ot[:r], sh[:r], ALU.add), reads=[okey, shkey], writes=[okey])


def emit_postres(P, stat, yt, ykey, xt, xkey, ot, okey, G, gkey, r):
    rs, rk = emit_rstd(P, stat, yt, ykey, ot, okey, r)
    P.op("dve", f_stt(ot[:r], yt[:r], rs[:r], G[:r], ALU.mult, ALU.mult), reads=[ykey, rk, gkey], writes=[okey])
    P.op("pool", f_tt(ot[:r], ot[:r], xt[:r], ALU.add), reads=[okey, xkey], writes=[okey])


def load_vec(P, tile, key, dram_t, off, eng="sp"):
    P.dma(eng, f_dma(tile[:], bcast_row(dram_t, off, D, 128)), writes=[key])


def emit_transposes(kb, src, skey, r, hT, hkey, col0, ident, psb):
    P = kb.P
    for q in range(4):
        b = psb[q % len(psb)]
        trs = [f_tr(kb.ps[b][:, j * 128:j * 128 + r], src[:r, (4 * q + j) * 128:(4 * q + j + 1) * 128], ident[:r, :r])
               for j in range(4)]
        P.op("pe", f_group(trs), reads=[skey, "ident"], writes=[("ps", b)])
        srcv = bass.AP(kb.ps[b], 0, [[512, 128], [128, 4], [1, r]])
        P.op("dve" if q % 2 == 0 else "act",
             (f_copy(hT[:, 4 * q:4 * q + 4, col0:col0 + r], srcv) if q % 2 == 0 else
              f_act(hT[:, 4 * q:4 * q + 4, col0:col0 + r], srcv, AF.Copy)),
             reads=[("ps", b)], writes=[(hkey, col0, q)])


BLOCKS = [(i * 128, 128) for i in range(8)] + [(1024, 32)]


def build_A():
    kb = KB()
    P = kb.P
    x = kb.din("x", [TOK, D])
    gain = kb.din("gain", [D])
    mod = kb.din("mod", [2, 6 * D])
    h = kb.dout("h", [TOK, D])
    stat = Stat(kb)
    A = [kb.sb("A%d" % i, [128, D]) for i in range(2)]
    SH = [kb.sb("SH%d" % i, [128, D]) for i in range(2)]
    tmp = kb.sb("tmp", [128, D])
    xt = [kb.sb("xt%d" % i, [128, D]) for i in range(2)]
    ot = [kb.sb("ot%d" % i, [128, D]) for i in range(2)]
    for c in range(2):
        load_vec(P, tmp, "tmp", gain, 0)
        load_vec(P, A[c], ("A", c), mod, c * 6 * D + D)
        P.op("dve", f_stt(A[c][:], A[c][:], 1.0, tmp[:], ALU.add, ALU.mult), reads=["tmp", ("A", c)], writes=[("A", c)])
        load_vec(P, SH[c], ("SH", c), mod, c * 6 * D)
    for bi, (t0, r) in enumerate(BLOCKS):
        c = 0 if bi < 8 else 1
        b = bi % 2
        P.dma("sp", f_dma(xt[b][:r], x.ap()[t0:t0 + r, :]), writes=[("xt", b)])
        emit_normmod(P, stat, xt[b], ("xt", b), ot[b], ("ot", b), A[c], ("A", c), SH[c], ("SH", c), r)
        P.dma("sp", f_dma(h.ap()[t0:t0 + r, :], ot[b][:r]), reads=[("ot", b)], out=True)
    return kb.finish()


def build_C(kind, last):
    kb = KB()
    P = kb.P
    x = kb.din("x", [TOK, D])
    yin = kb.din("yin", [TOK, D])
    wmix = kb.din("wmix", [D, 2 * D] if kind == "s5" else [D, D])
    w1 = kb.din("w1", [D, HID])
    w2 = kb.din("w2", [HID, D])
    mod = kb.din("mod", [2, 6 * D])
    g_post1 = kb.din("g_post1", [D])
    g_pre2 = kb.din("g_pre2", [D])
    g_post2 = kb.din("g_post2", [D])
    identd = kb.din("ident", [128, 128])
    x2 = kb.dout("x2", [TOK, D])
    if not last:
        modn = kb.din("modn", [2, 6 * D])
        g_pren = kb.din("g_pren", [D])
        hn = kb.dout("hn", [TOK, D])
    Y1 = kb.dscr("Y1", [TOK, D])
    X1 = kb.dscr("X1", [TOK, D])
    MQ = [kb.dscr("MQ%d" % q, [TOK, D]) for q in range(2)]

    stat = Stat(kb)
    ident = kb.sb("ident_sb", [128, 128])
    hT = kb.sb("hT", [128, 16, TOK], BF16)
    hid = kb.sb("hid", [128, 32, 544], BF16)
    NW = 3
    wb = [kb.sb("wbuf%d" % i, [128, 8192], BF16) for i in range(NW)]
    V = [kb.sb("V%d" % i, [128, D]) for i in range(4)]
    io = [kb.sb("io%d" % i, [128, D]) for i in range(4)]
    sm = [kb.sb("sm%d" % i, [128, 512]) for i in range(2)]
    kb.psum(8)
    P.dma("sp", f_dma(ident[:], identd.ap()), writes=["ident"])
    wctr = [0]

    def wnext():
        i = wctr[0] % NW
        wctr[0] += 1
        return i

    pctr = [0]

    def nextpb():
        i = 2 + pctr[0] % 6
        pctr[0] += 1
        return i

    def vec_prod(dst, dkey, gain_t, mod_off, c, plus_one):
        load_vec(P, io[3], ("io", 3), gain_t, 0)
        load_vec(P, dst, dkey, mod, c * 6 * D + mod_off)
        if plus_one:
            P.op("dve", f_stt(dst[:], dst[:], 1.0, io[3][:], ALU.add, ALU.mult), reads=[("io", 3), dkey], writes=[dkey])
        else:
            P.op("dve", f_tt(dst[:], dst[:], io[3][:], ALU.mult), reads=[("io", 3), dkey], writes=[dkey])

    for bi, (t0, r) in enumerate(BLOCKS):
        b = bi % 2
        P.dma("sp", f_dma(io[b][:r], yin.ap()[t0:t0 + r, :]), writes=[("io", b)])
        if kind == "s5":
            P.op("act", f_act(io[b][:r], io[b][:r], AF.Gelu), reads=[("io", b)], writes=[("io", b)])
        emit_transposes(kb, io[b], ("io", b), r, hT, "hT", t0, ident, [0, 1])

    if kind == "s5":
        nsl, cw = 8, 256
    else:
        nsl, cw = 4, 512
    for s in range(nsl):
        wi = wnext()
        wv = wb[wi][:, 0:16 * 512].rearrange("p (k n) -> p k n", k=16)
        if kind == "s5":
            for hh in range(2):
                src = wmix.ap()[:, hh * D + s * 256: hh * D + (s + 1) * 256].rearrange("(kc p) n -> p kc n", p=128)
                P.dma("pool", f_dma(wv[:, :, hh * 256:(hh + 1) * 256], src), writes=[("wb", wi, hh)])
            wkeys = [("wb", wi, 0), ("wb", wi, 1)]
        else:
            src = wmix.ap()[:, s * 512:(s + 1) * 512].rearrange("(kc p) n -> p kc n", p=128)
            P.dma("pool", f_dma(wv, src), writes=[("wb", wi, 0)])
            wkeys = [("wb", wi, 0)]
        for bi, (t0, r) in enumerate(BLOCKS):
            pb = nextpb()
            mms = [f_mm(kb.ps[pb][:r, :], hT[:, kt, t0:t0 + r], wv[:, kt, :], kt == 0, kt == 15) for kt in range(16)]
            P.op("pe", f_group(mms), reads=[("hT", t0, q_) for q_ in range(4)] + wkeys, writes=[("ps", pb)])
            sb_ = bi % 2
            if kind == "s5":
                P.op("act", f_act(sm[sb_][:r, 0:256], kb.ps[pb][:r, 256:512], AF.Sigmoid), reads=[("ps", pb)],
                     writes=[("sm", sb_, 0)])
                P.op("dve", f_tt(sm[sb_][:r, 256:512], kb.ps[pb][:r, 0:256], sm[sb_][:r, 0:256], ALU.mult),
                     reads=[("ps", pb), ("sm", sb_, 0)], writes=[("sm", sb_, 1)])
                P.dma("sp", f_dma(Y1.ap()[t0:t0 + r, s * 256:(s + 1) * 256], sm[sb_][:r, 256:512]),
                      reads=[("sm", sb_, 1)], writes=[("Y1", bi, s)])
            else:
                P.op("act", f_act(sm[sb_][:r, :], kb.ps[pb][:r, :], AF.Copy), reads=[("ps", pb)], writes=[("sm", sb_, 1)])
                P.dma("sp", f_dma(Y1.ap()[t0:t0 + r, s * 512:(s + 1) * 512], sm[sb_][:r, :]),
                      reads=[("sm", sb_, 1)], writes=[("Y1", bi, s)])

    for c in range(2):
        vec_prod(V[0], ("V", 0), g_post1, 2 * D, c, False)
        vec_prod(V[1], ("V", 1), g_pre2, 4 * D, c, True)
        load_vec(P, V[2], ("V", 2), mod, c * 6 * D + 3 * D)
        for bi, (t0, r) in enumerate(BLOCKS):
            if (bi < 8) != (c == 0):
                continue
            P.dma("sp", f_dma(io[0][:r], Y1.ap()[t0:t0 + r, :]), reads=[("Y1", bi, s) for s in range(nsl)],
                  writes=[("io", 0)])
            P.dma("sp", f_dma(io[1][:r], x.ap()[t0:t0 + r, :]), writes=[("io", 1)])
            emit_postres(P, stat, io[0], ("io", 0), io[1], ("io", 1), io[2], ("io", 2), V[0], ("V", 0), r)
            P.dma("sp", f_dma(X1.ap()[t0:t0 + r, :], io[2][:r]), reads=[("io", 2)], writes=[("X1", bi)])
            emit_normmod(P, stat, io[2], ("io", 2), io[0], ("io", 0), V[1], ("V", 1), V[2], ("V", 2), r)
            emit_transposes(kb, io[0], ("io", 0), r, hT, "hT", t0, ident, [0, 1])

    halves = [(0, 512, [0, 1, 2, 3]), (512, 544, [4, 5, 6, 7, 8])]
    for (c0, ncol, blks) in halves:
        chunks = [(c0, 512)] if ncol == 512 else [(c0, 512), (c0 + 512, 32)]
        hkeys = [("hT", BLOCKS[bi][0], q_) for bi in blks for q_ in range(4)]
        for q in range(2):
            for s in range(8):
                wi = wnext()
                wv = wb[wi][:, 0:16 * 512].rearrange("p (k n) -> p k n", k=16)
                col = q * 4096 + s * 512
                src = w1.ap()[:, col:col + 512].rearrange("(kc p) n -> p kc n", p=128)
                P.dma("pool", f_dma(wv, src), writes=[("wb", wi, 0)])
                for ht in range(4):
                    for (cc, n) in chunks:
                        pb = nextpb()
                        mms = [f_mm(kb.ps[pb][:, :n], wv[:, kt, ht * 128:(ht + 1) * 128], hT[:, kt, cc:cc + n],
                                    kt == 0, kt == 15) for kt in range(16)]
                        P.op("pe", f_group(mms), reads=hkeys + [("wb", wi, 0)], writes=[("ps", pb)])
                        sb_ = (s * 4 + ht) % 2
                        P.op("act", f_act(sm[sb_][:, :n], kb.ps[pb][:, :n], AF.Relu), reads=[("ps", pb)],
                             writes=[("sm", sb_, 1)])
                        P.op("dve", f_tt(hid[:, s * 4 + ht, cc - c0:cc - c0 + n], sm[sb_][:, :n], sm[sb_][:, :n], ALU.mult),
                             reads=[("sm", sb_, 1)], writes=[("hid", s * 4 + ht, cc)])
            hidkeys = [("hid", t, cc) for t in range(32) for (cc, n) in chunks]
            for s2 in range(8):
                wi = wnext()
                wv = wb[wi][:, 0:32 * 256].rearrange("p (k n) -> p k n", k=32)
                src = w2.ap()[q * 4096:(q + 1) * 4096, s2 * 256:(s2 + 1) * 256].rearrange("(kc p) n -> p kc n", p=128)
                P.dma("pool", f_dma(wv, src), writes=[("wb", wi, 0)])
                for bi in blks:
                    t0, r = BLOCKS[bi]
                    pb = nextpb()
                    mms = [f_mm(kb.ps[pb][:r, 0:256], hid[:, kt, t0 - c0:t0 - c0 + r], wv[:, kt, :], kt == 0, kt == 31)
                           for kt in range(32)]
                    P.op("pe", f_group(mms), reads=hidkeys + [("wb", wi, 0)], writes=[("ps", pb)])
                    sb_ = bi % 2
                    P.op("act", f_act(sm[sb_][:r, 0:256], kb.ps[pb][:r, 0:256], AF.Copy), reads=[("ps", pb)],
                         writes=[("sm", sb_, 1)])
                    P.dma("sp", f_dma(MQ[q].ap()[t0:t0 + r, s2 * 256:(s2 + 1) * 256], sm[sb_][:r, 0:256]),
                          reads=[("sm", sb_, 1)], writes=[("MQ", q, bi, s2)])

    for c in range(2):
        vec_prod(V[0], ("V", 0), g_post2, 5 * D, c, False)
        if not last:
            load_vec(P, io[3], ("io", 3), g_pren, 0)
            load_vec(P, V[1], ("V", 1), modn, c * 6 * D + D)
            P.op("dve", f_stt(V[1][:], V[1][:], 1.0, io[3][:], ALU.add, ALU.mult), reads=[("io", 3), ("V", 1)],
                 writes=[("V", 1)])
            load_vec(P, V[2], ("V", 2), modn, c * 6 * D)
        for bi, (t0, r) in enumerate(BLOCKS):
            if (bi < 8) != (c == 0):
                continue
            P.dma("sp", f_dma(io[0][:r], MQ[0].ap()[t0:t0 + r, :]), reads=[("MQ", 0, bi, s) for s in range(8)],
                  writes=[("io", 0)])
            P.dma("sp", f_dma(io[1][:r], MQ[1].ap()[t0:t0 + r, :]), reads=[("MQ", 1, bi, s) for s in range(8)],
                  writes=[("io", 1)])
            P.op("pool", f_tt(io[0][:r], io[0][:r], io[1][:r], ALU.add), reads=[("io", 0), ("io", 1)], writes=[("io", 0)])
            P.dma("sp", f_dma(io[1][:r], X1.ap()[t0:t0 + r, :]), reads=[("X1", bi)], writes=[("io", 1)])
            emit_postres(P, stat, io[0], ("io", 0), io[1], ("io", 1), io[2], ("io", 2), V[0], ("V", 0), r)
            P.dma("sp", f_dma(x2.ap()[t0:t0 + r, :], io[2][:r]), reads=[("io", 2)], out=True)
            if not last:
                emit_normmod(P, stat, io[2], ("io", 2), io[0], ("io", 0), V[1], ("V", 1), V[2], ("V", 2), r)
                P.dma("sp", f_dma(hn.ap()[t0:t0 + r, :], io[0][:r]), reads=[("io", 0)], out=True)
    return kb.finish()


S5_PD = 1072
S5_OFF_D = 2 * S5_PD
S5_OFF_SGN = S5_OFF_D + 256
S5_OFF_MF = S5_OFF_SGN + 2
S5_OFF_MR = S5_OFF_MF + 128
S5_OFF_ID = S5_OFF_MR + 128
S5_NP = S5_OFF_ID + 128
S5_NCH = NTOT // 8
S5_BLOCKS = [(0, 32)] + [(32 + 128 * i, 128) for i in range(8)]


def build_BS5(dbg=False):
    kb = KB()
    P = kb.P
    dbgo = {}

    def dump(name, tile_ap, shape, keys, dt=F32):
        if not dbg:
            return
        o = kb.dout('dbg_' + name, shape, dt)
        P.dma('sp', f_dma(o.ap(), tile_ap), reads=keys, out=True)
    uT_d = kb.din("uT", [128, 16, S5_NCH])
    ucm_d = kb.din("ucm", [S5_NCH, 16 * 128])
    prm_d = kb.din("prm", [128, S5_NP])
    y_d = kb.dout("y", [S5_NCH, 16 * 128])

    PR = kb.sb("PR", [128, S5_NP])
    uT = kb.sb("uTb", [128, 16, S5_NCH], BF16)
    TB = kb.sb("TB", [128, 48, 16])
    TBI = kb.sb("TBI", [128, 4, 16], mybir.dt.int32)
    KEEP = kb.sb("KEEP", [128, 4, 16])
    ZT = [kb.sb("ZT%d" % i, [128, 16, 16]) for i in range(12)]
    BIG = [kb.sb("BIG%d" % i, [128, 16, 128]) for i in range(5)]
    W1 = kb.sb("W1", [128, 16, 128], BF16)
    W1s = kb.sb("W1s", [128, 16, 128], BF16)
    MG = kb.sb("MG", [128, 16, 128], BF16)
    CG = kb.sb("CG", [128, 16, 128], BF16)
    COEF = kb.sb("COEF", [128, 64])
    LOC = [kb.sb("LOC%d" % i, [128, 128, 32]) for i in range(2)]
    SIN = [kb.sb("SIN%d" % i, [128, 16, 128], BF16) for i in range(2)]
    ZS = [kb.sb("ZS%d" % i, [128, 48]) for i in range(2)]
    TS = [kb.sb("TS%d" % i, [128, 32]) for i in range(6)]
    UD = [kb.sb("UD%d" % i, [128, 2048]) for i in range(2)]
    YB = [kb.sb("YB%d" % i, [128, 512]) for i in range(4)]
    kb.psum(8)

    P.dma("sp", f_dma(PR[:], prm_d.ap()), writes=["PR"])
    P.dma("pool", f_dma(uT[:], uT_d.ap()), writes=["uT"])
    ident = PR[:, S5_OFF_ID:S5_OFF_ID + 128]
    sgnA = PR[:, S5_OFF_SGN:S5_OFF_SGN + 1]
    sgnB = PR[:, S5_OFF_SGN + 1:S5_OFF_SGN + 2]
    dvec_b = bass.AP(PR, S5_OFF_D, [[S5_NP, 128], [16, 16], [0, 8], [1, 16]])

    tbc = [0]

    def tmp():
        i = tbc[0] % 48
        tbc[0] += 1
        return TB[:, i, :], ("tb", i)

    def dv(out, ok, fn_reads):
        pass

    def mul(o, ok, a, ak, b, bk, eng="dve"):
        P.op(eng, f_tt(o, a, b, ALU.mult), reads=[ak, bk], writes=[ok])

    def add(o, ok, a, ak, b, bk, eng="dve"):
        P.op(eng, f_tt(o, a, b, ALU.add), reads=[ak, bk], writes=[ok])

    def sub(o, ok, a, ak, b, bk, eng="dve"):
        P.op(eng, f_tt(o, a, b, ALU.subtract), reads=[ak, bk], writes=[ok])

    def cmul(orr, ork, oi, oik, xr, xrk, xi, xik, yr, yrk, yi, yik):
        t1, k1 = tmp(); t2, k2 = tmp(); t3, k3 = tmp(); t4, k4 = tmp()
        mul(t1, k1, xr, xrk, yr, yrk)
        mul(t2, k2, xi, xik, yi, yik)
        mul(t3, k3, xr, xrk, yi, yik)
        mul(t4, k4, xi, xik, yr, yrk)
        sub(orr, ork, t1, k1, t2, k2)
        add(oi, oik, t3, k3, t4, k4)

    def range_reduce(v, vk, shift):
        a, ak = tmp()
        P.op("dve", f_ts(a, v, 1.0, shift, ALU.mult, ALU.add), reads=[vk], writes=[ak])
        kf, kfk = tmp()
        P.op("dve", f_ts(kf, a, 1.0 / TWO_PI), reads=[ak], writes=[kfk])
        ii = tbc[0] % 4
        ki, kik = TBI[:, ii, :], ("tbi", ii)
        P.op("dve", f_copy(ki, kf), reads=[kfk], writes=[kik])
        kf2, kf2k = tmp()
        P.op("dve", f_copy(kf2, ki), reads=[kik], writes=[kf2k])
        r, rk = tmp()
        P.op("dve", f_stt(r, kf2, -TWO_PI, a, ALU.mult, ALU.add), reads=[kf2k, ak], writes=[rk])
        m1, m1k = tmp()
        P.op("dve", f_ts(m1, r, -np.pi, None, ALU.is_lt), reads=[rk], writes=[m1k])
        r2, r2k = tmp()
        P.op("dve", f_stt(r2, m1, TWO_PI, r, ALU.mult, ALU.add), reads=[m1k, rk], writes=[r2k])
        m2, m2k = tmp()
        P.op("dve", f_ts(m2, r2, np.pi, None, ALU.is_gt), reads=[r2k], writes=[m2k])
        r3, r3k = tmp()
        P.op("dve", f_stt(r3, m2, -TWO_PI, r2, ALU.mult, ALU.add), reads=[m2k, r2k], writes=[r3k])
        r4, r4k = tmp()
        P.op("dve", f_ts(r4, r3, -3.14159, 3.14159, ALU.max, ALU.min), reads=[r3k], writes=[r4k])
        return r4, r4k

    pctr = [0]

    def bankA():
        i = pctr[0] % 4
        pctr[0] += 1
        return i

    qctr = [0]

    def bankC():
        i = 4 + qctr[0] % 4
        qctr[0] += 1
        return i

    for d in range(2):
        po = d * S5_PD
        ar, ai, ldt = PR[:, po:po + 16], PR[:, po + 16:po + 32], PR[:, po + 32:po + 48]
        Bs1 = PR[:, po + 48:po + 304].rearrange("p (g h) -> p g h", g=16)
        Bs2 = PR[:, po + 304:po + 560].rearrange("p (g h) -> p g h", g=16)
        Cs1 = PR[:, po + 560:po + 816].rearrange("p (g h) -> p g h", g=16)
        Cs2 = PR[:, po + 816:po + 1072].rearrange("p (g h) -> p g h", g=16)
        dt_, dtk = tmp()
        P.op("act", f_act(dt_, ldt, AF.Exp), reads=["PR"], writes=[dtk])
        ardt, ardtk = tmp()
        mul(ardt, ardtk, ar, "PR", dt_, dtk)
        mag, magk = tmp()
        P.op("act", f_act(mag, ardt, AF.Exp), reads=[ardtk], writes=[magk])
        th, thk = tmp()
        mul(th, thk, ai, "PR", dt_, dtk)
        rs_, rsk = range_reduce(th, thk, 0.0)
        rc_, rck = range_reduce(th, thk, np.pi / 2)
        sn, snk = tmp()
        P.op("act", f_act(sn, rs_, AF.Sin), reads=[rsk], writes=[snk])
        cs, csk = tmp()
        P.op("act", f_act(cs, rc_, AF.Sin), reads=[rck], writes=[csk])
        Zr, Zi, ZRr, ZRi, Gr, Gi, GRr, GRi, S1, S2, S3, S4 = ZT
        zk = lambda nm, k: ("z", nm, k)
        Ar, Ark = Zr[:, :, 8], zk("Zr", 8)
        Ai, Aik = Zi[:, :, 8], zk("Zi", 8)
        mul(Ar, Ark, mag, magk, cs, csk)
        mul(Ai, Aik, mag, magk, sn, snk)
        nr, nrk = tmp()
        P.op("dve", f_ts(nr, Ar, -1.0, None, ALU.add), reads=[Ark], writes=[nrk])
        t1, k1 = tmp(); t2, k2 = tmp(); den, denk = tmp(); rden, rdenk = tmp()
        mul(t1, k1, ar, "PR", ar, "PR")
        mul(t2, k2, ai, "PR", ai, "PR")
        add(den, denk, t1, k1, t2, k2)
        P.op("dve", f_recip(rden, den), reads=[denk], writes=[rdenk])
        t3, k3 = tmp(); t4, k4 = tmp(); t5, k5 = tmp(); t6, k6 = tmp()
        mul(t3, k3, nr, nrk, ar, "PR")
        mul(t4, k4, Ai, Aik, ai, "PR")
        add(t5, k5, t3, k3, t4, k4)
        gr, grk = KEEP[:, 0, :], ('keep', 0)
        mul(gr, grk, t5, k5, rden, rdenk)
        mul(t6, k6, Ai, Aik, ar, "PR")
        t7, k7 = tmp(); t8, k8 = tmp()
        mul(t7, k7, nr, nrk, ai, "PR")
        sub(t8, k8, t6, k6, t7, k7)
        gi, gik = KEEP[:, 1, :], ('keep', 1)
        mul(gi, gik, t8, k8, rden, rdenk)
        u1, uk1 = tmp(); u2, uk2 = tmp(); m2, m2k = tmp(); rm, rmk = tmp()
        mul(u1, uk1, Ar, Ark, Ar, Ark)
        mul(u2, uk2, Ai, Aik, Ai, Aik)
        add(m2, m2k, u1, uk1, u2, uk2)
        P.op("dve", f_recip(rm, m2), reads=[m2k], writes=[rmk])
        mul(Zr[:, :, 6], zk("Zr", 6), Ar, Ark, rm, rmk)
        nb_, nbk = tmp()
        mul(nb_, nbk, Ai, Aik, rm, rmk)
        P.op("dve", f_ts(Zi[:, :, 6], nb_, -1.0), reads=[nbk], writes=[zk("Zi", 6)])
        P.op("pool", f_memset(Zr[:, :, 7], 1.0), writes=[zk("Zr", 7)])
        P.op("pool", f_memset(Zi[:, :, 7], 0.0), writes=[zk("Zi", 7)])
        for k in range(9, 16):
            cmul(Zr[:, :, k], zk("Zr", k), Zi[:, :, k], zk("Zi", k), Zr[:, :, k - 1], zk("Zr", k - 1), Zi[:, :, k - 1],
                 zk("Zi", k - 1), Ar, Ark, Ai, Aik)
        for k in range(5, -1, -1):
            cmul(Zr[:, :, k], zk("Zr", k), Zi[:, :, k], zk("Zi", k), Zr[:, :, k + 1], zk("Zr", k + 1), Zi[:, :, k + 1],
                 zk("Zi", k + 1), Zr[:, :, 6], zk("Zr", 6), Zi[:, :, 6], zk("Zi", 6))
        allz = lambda nm: [zk(nm, k) for k in range(16)]
        grb = gr.unsqueeze(2).broadcast_to([128, 16, 16])
        gib = gi.unsqueeze(2).broadcast_to([128, 16, 16])
        P.op("dve", f_tt(S1[:], Zr[:], grb, ALU.mult), reads=allz("Zr") + [grk], writes=["S1"])
        P.op("dve", f_tt(S2[:], Zi[:], gib, ALU.mult), reads=allz("Zi") + [gik], writes=["S2"])
        P.op("dve", f_tt(Gr[:], S1[:], S2[:], ALU.subtract), reads=["S1", "S2"], writes=["Gr"])
        P.op("dve", f_tt(S1[:], Zr[:], gib, ALU.mult), reads=allz("Zr") + [gik], writes=["S1"])
        P.op("dve", f_tt(S2[:], Zi[:], grb, ALU.mult), reads=allz("Zi") + [grk], writes=["S2"])
        P.op("dve", f_tt(Gi[:], S1[:], S2[:], ALU.add), reads=["S1", "S2"], writes=["Gi"])
        if d == 0:
            for k in range(16):
                P.op("pool", f_copy(GRr[:, :, k], Gr[:, :, 15 - k]), reads=["Gr"], writes=[("GRr", k)])
                P.op("pool", f_copy(GRi[:, :, k], Gi[:, :, 15 - k]), reads=["Gi"], writes=[("GRi", k)])
            TZr, TZrk, TZi, TZik = Zr, allz("Zr"), Zi, allz("Zi")
            TGr, TGrk = GRr, [("GRr", k) for k in range(16)]
            TGi, TGik = GRi, [("GRi", k) for k in range(16)]
            w1s, ys, cs_ = slice(1, 9), slice(0, 8), slice(8, 16)
        else:
            for k in range(16):
                P.op("pool", f_copy(ZRr[:, :, k], Zr[:, :, 15 - k]), reads=[zk("Zr", 15 - k)], writes=[("ZRr", k)])
                P.op("pool", f_copy(ZRi[:, :, k], Zi[:, :, 15 - k]), reads=[zk("Zi", 15 - k)], writes=[("ZRi", k)])
            TZr, TZrk, TZi, TZik = ZRr, [("ZRr", k) for k in range(16)], ZRi, [("ZRi", k) for k in range(16)]
            TGr, TGrk, TGi, TGik = Gr, ["Gr"], Gi, ["Gi"]
            w1s, ys, cs_ = slice(7, 15), slice(8, 16), slice(0, 8)
        if d == 0:
            dump('Zr', Zr[:], [128, 16, 16], allz('Zr'))
            dump('Zi', Zi[:], [128, 16, 16], allz('Zi'))
            dump('Gr', Gr[:], [128, 16, 16], ['Gr'])
            dump('Gi', Gi[:], [128, 16, 16], ['Gi'])
        P.op("dve", f_ts(S1[:], TGi[:], sgnB), reads=TGik + ["PR"], writes=["S1"])
        P.op("dve", f_ts(S2[:], TGi[:], sgnA), reads=TGik + ["PR"], writes=["S2"])
        P.op("dve", f_ts(S3[:], TZr[:], sgnA), reads=TZrk + ["PR"], writes=["S3"])
        P.op("dve", f_ts(S4[:], TZi[:], -1.0), reads=TZik, writes=["S4"])

        def build(dst, dkey, s1, s1k, sl, T1, s2, s2k, T2):
            a_ = s1[:, :, sl].unsqueeze(3).broadcast_to([128, 16, 8, 16])
            b_ = T1.unsqueeze(2).broadcast_to([128, 16, 8, 16])
            c_ = s2[:, :, sl].unsqueeze(3).broadcast_to([128, 16, 8, 16])
            d_ = T2.unsqueeze(2).broadcast_to([128, 16, 8, 16])
            v = lambda t: t[:].rearrange("p g (j h) -> p g j h", j=8)
            P.op("dve", f_tt(v(BIG[3]), a_, b_, ALU.mult), reads=s1k + ["PR"], writes=["BIG3"])
            P.op("dve", f_tt(v(BIG[4]), c_, d_, ALU.mult), reads=s2k + ["PR"], writes=["BIG4"])
            P.op("dve", f_tt(dst[:], BIG[3][:], BIG[4][:], ALU.add), reads=["BIG3", "BIG4"], writes=[dkey])

        XP, XS, YT = BIG[0], BIG[1], BIG[2]
        build(XP, "XP", TGr, TGrk, w1s, Bs1, S1, ["S1"], Bs2)
        build(XS, "XS", TGr, TGrk, w1s, Bs2, S2, ["S2"], Bs1)
        build(YT, "YT", S3, ["S3"], ys, Cs1, S4, ["S4"], Cs2)
        build(CG, ("CG", d), S3, ["S3"], cs_, Cs1, S4, ["S4"], Cs2)
        if d == 0:
            dump('XP', XP[:], [128, 16, 128], ['XP'])
            dump('YT', YT[:], [128, 16, 128], ['YT'])
            dump('CG', CG[:], [128, 16, 128], [('CG', 0)], BF16)
        mask = PR[:, S5_OFF_MF:S5_OFF_MF + 128] if d == 0 else PR[:, S5_OFF_MR:S5_OFF_MR + 128]
        maskb = bass.AP(PR, S5_OFF_MF if d == 0 else S5_OFF_MR, [[S5_NP, 128], [0, 4], [1, 128]])
        for gq in range(4):
            b = bankA()
            trs = [f_tr(kb.ps[b][:, gl * 128:(gl + 1) * 128], XP[:, 4 * gq + gl, :], ident) for gl in range(4)]
            P.op("pe", f_group(trs), reads=["XP", "PR"], writes=[("ps", b)])
            P.op("act", f_act(W1[:, 4 * gq:4 * gq + 4, :], kb.ps[b][:].rearrange("p (g c) -> p g c", g=4), AF.Copy),
                 reads=[("ps", b)], writes=[("W1", gq)])
            b = bankA()
            trs = [f_tr(kb.ps[b][:, gl * 128:(gl + 1) * 128], XS[:, 4 * gq + gl, :], ident) for gl in range(4)]
            P.op("pe", f_group(trs), reads=["XS", "PR"], writes=[("ps", b)])
            P.op("act", f_act(W1s[:, 4 * gq:4 * gq + 4, :], kb.ps[b][:].rearrange("p (g c) -> p g c", g=4), AF.Copy),
                 reads=[("ps", b)], writes=[("W1s", gq)])
            b = bankA()
            mms = [f_mm(kb.ps[b][:, gl * 128:(gl + 1) * 128], XP[:, 4 * gq + gl, :], YT[:, 4 * gq + gl, :]) for gl in range(4)]
            P.op("pe", f_group(mms), reads=["XP", "YT"], writes=[("ps", b)])
            P.op("dve", f_tt(MG[:, 4 * gq:4 * gq + 4, :], kb.ps[b][:].rearrange("p (g c) -> p g c", g=4), maskb, ALU.mult),
                 reads=[("ps", b), "PR"], writes=[("MG", gq)])
        if d == 0:
            dump('MG', MG[:], [128, 16, 128], [('MG', q_) for q_ in range(4)], BF16)
            dump('W1', W1[:], [128, 16, 128], [('W1', q_) for q_ in range(4)], BF16)
            dump('W1s', W1s[:], [128, 16, 128], [('W1s', q_) for q_ in range(4)], BF16)
        AR2, AIS2 = COEF[:, 0:32], COEF[:, 32:64]
        P.op("dve", f_copy(COEF[:, 0:16], Zr[:, :, 15]), reads=[zk("Zr", 15)], writes=[("coef", 0)])
        P.op("dve", f_copy(COEF[:, 16:32], Zr[:, :, 15]), reads=[zk("Zr", 15)], writes=[("coef", 1)])
        P.op("dve", f_ts(COEF[:, 32:48], Zi[:, :, 15], sgnB), reads=[zk("Zi", 15), "PR"], writes=[("coef", 2)])
        P.op("dve", f_ts(COEF[:, 48:64], Zi[:, :, 15], sgnA), reads=[zk("Zi", 15), "PR"], writes=[("coef", 3)])
        ckeys = [("coef", i) for i in range(4)]
        order = list(range(9)) if d == 0 else [0] + list(range(8, 0, -1))
        zcur = 0
        P.op("dve", f_memset(ZS[0][:], 0.0), writes=[("ZS", 0)])
        w1keys = [("W1", gq) for gq in range(4)]
        w1skeys = [("W1s", gq) for gq in range(4)]
        for bi_, bidx in enumerate(order):
            c0, R = S5_BLOCKS[bidx]
            lb = bi_ % 2
            for half, (Wt, wk) in enumerate(((W1, w1keys), (W1s, w1skeys))):
                for gq in range(4):
                    b = bankA()
                    mms = [f_mm(kb.ps[b][:, gl * 128:gl * 128 + R], Wt[:, 4 * gq + gl, :], uT[:, 4 * gq + gl, c0:c0 + R])
                           for gl in range(4)]
                    P.op("pe", f_group(mms), reads=wk + ["uT"], writes=[("ps", b)])
                    dst = bass.AP(LOC[lb], half * 16 + 4 * gq, [[4096, 128], [1, 4], [32, R]])
                    src = bass.AP(kb.ps[b], 0, [[512, 128], [128, 4], [1, R]])
                    P.op("act", f_act(dst, src, AF.Copy), reads=[("ps", b)], writes=[("LOC", lb, half, gq)])
            lockeys = [("LOC", lb, hf, gq) for hf in range(2) for gq in range(4)]
            ns = range(R) if d == 0 else range(R - 1, -1, -1)
            for n in ns:
                Zc, Zck = ZS[zcur], ("ZS", zcur)
                Zn, Znk = ZS[1 - zcur], ("ZS", 1 - zcur)
                P.op("act", f_act(SIN[lb][:, :, n], Zc[:, 0:16], AF.Copy), reads=[Zck], writes=[("SIN", lb, n)])
                ta, tak = TS[(2 * n) % 6], ("TS", (2 * n) % 6)
                tb_, tbk = TS[(2 * n + 1) % 6], ("TS", (2 * n + 1) % 6)
                P.op("dve", f_tt(ta[:], Zc[:, 0:32], AR2, ALU.mult), reads=[Zck] + ckeys, writes=[tak])
                P.op("dve", f_tt(tb_[:], Zc[:, 16:48], AIS2, ALU.mult), reads=[Zck] + ckeys, writes=[tbk])
                P.op("dve", f_tt(ta[:], ta[:], LOC[lb][:, n, :], ALU.add), reads=[tak] + (lockeys if n == ns[0] else []),
                     writes=[tak])
                P.op("dve", f_tt(Zn[:, 0:32], ta[:], tb_[:], ALU.add), reads=[tak, tbk], writes=[Znk])
                P.op("act", f_act(Zn[:, 32:48], Zn[:, 0:16], AF.Copy), reads=[Znk], writes=[Znk])
                zcur = 1 - zcur
            sinkeys = [("SIN", lb, n) for n in range(R)]
            if d == 0 and bi_ == 0:
                dump('LOC', LOC[lb][:], [128, 128, 32], lockeys)
                dump('SIN', SIN[lb][:], [128, 16, 128], sinkeys, BF16)
                dump('COEF', COEF[:], [128, 64], ckeys)
            ub = bi_ % 2
            if d == 0:
                P.dma("sp", f_dma(UD[ub][:R], ucm_d.ap()[c0:c0 + R, :]), writes=[("UD", ub)])
                P.op("pool", f_tt(UD[ub][:R].rearrange("p (g j h) -> p g j h", g=16, j=8), UD[ub][:R].rearrange("p (g j h) -> p g j h", g=16, j=8),
                                  bass.AP(PR, S5_OFF_D, [[S5_NP, R], [16, 16], [0, 8], [1, 16]]), ALU.mult),
                     reads=[("UD", ub), "PR"], writes=[("UD", ub)])
            for gq in range(4):
                b = bankC()
                mms = []
                for gl in range(4):
                    g = 4 * gq + gl
                    mms.append(f_mm(kb.ps[b][:R, gl * 128:(gl + 1) * 128], uT[:, g, c0:c0 + R], MG[:, g, :], True, False))
                    mms.append(f_mm(kb.ps[b][:R, gl * 128:(gl + 1) * 128], SIN[lb][:, g, 0:R], CG[:, g, :], False, True))
                P.op("pe", f_group(mms), reads=["uT", ("MG", gq), ("CG", d)] + sinkeys, writes=[("ps", b)])
                yb = qctr[0] % 4
                P.op("act", f_act(YB[yb][:R, :], kb.ps[b][:R, :], AF.Copy), reads=[("ps", b)], writes=[("YB", yb)])
                if d == 0:
                    P.op("pool", f_tt(YB[yb][:R, :], YB[yb][:R, :], UD[ub][:R, gq * 512:(gq + 1) * 512], ALU.add),
                         reads=[("YB", yb), ("UD", ub)], writes=[("YB", yb)])
                    P.dma("sp", f_dma(y_d.ap()[c0:c0 + R, gq * 512:(gq + 1) * 512], YB[yb][:R, :]), reads=[("YB", yb)],
                          writes=[("y", bidx, gq)], out=True)
                else:
                    P.dma("pool", (lambda o_, i_: (lambda e: e.dma_start(out=o_, in_=i_, accum_op=ALU.add)))(
                        y_d.ap()[c0:c0 + R, gq * 512:(gq + 1) * 512], YB[yb][:R, :]),
                        reads=[("YB", yb), ("y", bidx, gq)], writes=[("y", bidx, gq)], out=True)
    return kb.finish()


def s5_prm(inp, j, core):
    gs = slice(16 * core, 16 * core + 16)
    prm = np.zeros((128, S5_NP), np.float32)
    for d in range(2):
        po = d * S5_PD
        ar = inp["s5_a_re"][j, d, gs].T
        ai = inp["s5_a_im"][j, d, gs].T
        prm[:, po:po + 16] = np.concatenate([ar, ar])
        prm[:, po + 16:po + 32] = np.concatenate([ai, ai])
        prm[:, po + 32:po + 48] = inp["s5_log_dt"][j, d, gs][None, :]
        br = inp["s5_b_re"][j, d, gs].transpose(1, 0, 2).reshape(64, 256)
        bi = inp["s5_b_im"][j, d, gs].transpose(1, 0, 2).reshape(64, 256)
        cr = inp["s5_c_re"][j, d, gs].transpose(2, 0, 1).reshape(64, 256)
        ci = inp["s5_c_im"][j, d, gs].transpose(2, 0, 1).reshape(64, 256)
        prm[:, po + 48:po + 304] = np.concatenate([br, bi])
        prm[:, po + 304:po + 560] = np.concatenate([bi, br])
        prm[:, po + 560:po + 816] = np.concatenate([cr, ci])
        prm[:, po + 816:po + 1072] = np.concatenate([ci, cr])
    prm[:, S5_OFF_D:S5_OFF_D + 256] = inp["s5_d"][j, 256 * core:256 * core + 256][None, :]
    prm[:64, S5_OFF_SGN] = 1.0
    prm[64:, S5_OFF_SGN] = -1.0
    prm[:64, S5_OFF_SGN + 1] = -1.0
    prm[64:, S5_OFF_SGN + 1] = 1.0
    ii = np.arange(128) // 16
    prm[:, S5_OFF_MF:S5_OFF_MF + 128] = (ii[:, None] <= ii[None, :])
    prm[:, S5_OFF_MR:S5_OFF_MR + 128] = (ii[:, None] >= ii[None, :])
    prm[:, S5_OFF_ID:S5_OFF_ID + 128] = np.eye(128)
    return prm


def s5_inputs(h_seq, inp, j):
    maps = []
    for core in range(NC):
        u = h_seq[:, 256 * core:256 * core + 256].reshape(S5_NCH, 8, 16, 16)
        uT = np.ascontiguousarray(u.transpose(1, 3, 2, 0)).reshape(128, 16, S5_NCH)
        ucm = np.ascontiguousarray(u.transpose(0, 2, 1, 3)).reshape(S5_NCH, 2048)
        maps.append({"uT": uT, "ucm": ucm, "prm": s5_prm(inp, j, core)})
    return maps


def s5_gather(results):
    out = np.empty((NTOT, D), np.float32)
    for core, r in enumerate(results):
        y = r["y"].reshape(S5_NCH, 16, 8, 16).transpose(0, 2, 1, 3).reshape(NTOT, 256)
        out[:, 256 * core:256 * core + 256] = y
    return out


HG_OFF_LB = 0
HG_OFF_ON = 8
HG_OFF_MF = HG_OFF_ON + 256
HG_OFF_MR = HG_OFF_MF + 128
HG_OFF_SM = HG_OFF_MR + 128
HG_OFF_ID = HG_OFF_SM + 256
HG_OFF_RM = HG_OFF_ID + 128
HG_NP = HG_OFF_RM + 4
HG_NBLK = NTOT // 128
HG_NGRP = NTOT // 256


def build_BHG(jlayer):
    kb = KB()
    P = kb.P
    hT_d = kb.din("hT", [D, NTOT])
    w5_d = kb.din("w5", [D, 1280])
    prm_d = kb.din("prm", [128, HG_NP])
    og_d = kb.dout("og", [NTOT, 256])
    QBd = [kb.dscr("QB%d" % d, [2, 128, NTOT], BF16) for d in range(2)]
    KBd = [kb.dscr("KB%d" % d, [2, 128, NTOT], BF16) for d in range(2)]
    KEd = [kb.dscr("KE%d" % d, [NTOT, 256], BF16) for d in range(2)]
    VVd = kb.dscr("VV", [NTOT, 256], BF16)
    SGd = kb.dscr("SG", [NTOT, 256])
    OFd = kb.dscr("OF", [NTOT, 256])

    PR = kb.sb("PR", [128, HG_NP])
    wsb = kb.sb("wsb", [128, 16, 1280], BF16)
    hTg = [kb.sb("hTg%d" % i, [128, 16, 256], BF16) for i in range(2)]
    LB = kb.sb("LB", [128, 32])
    FF = kb.sb("FF", [128, 2, 2, HG_NBLK * 4])
    NW = 24
    WK = [kb.sb("WK%d" % i, [128, 256]) for i in range(NW)]
    WB = [kb.sb("WBF%d" % i, [128, 256], BF16) for i in range(8)]
    stat = Stat(kb, 6)
    kb.psum(8)
    P.dma("sp", f_dma(PR[:], prm_d.ap()), writes=["PR"])
    P.dma("pool", f_dma(wsb[:], w5_d.ap().rearrange("(kc p) n -> p kc n", p=128)), writes=["wsb"])
    ident = PR[:, HG_OFF_ID:HG_OFF_ID + 128]
    smask = PR[:, HG_OFF_SM:HG_OFF_SM + 256]

    raw = PR[:, 0:8].rearrange("p (d l h) -> p d l h", d=2, l=2)
    E = LB[:, 8:16].rearrange("p (d l h) -> p d l h", d=2, l=2)
    P.op("act", f_act(LB[:, 8:16], PR[:, 0:8], AF.Exp), reads=["PR"], writes=["lbE"])
    ssum = LB[:, 16:20].rearrange("p (d h) -> p d h", d=2)
    P.op("dve", f_tt(ssum, E[:, :, 0, :], E[:, :, 1, :], ALU.add), reads=["lbE"], writes=["lbS"])
    P.op("dve", f_recip(LB[:, 20:24], LB[:, 16:20]), reads=["lbS"], writes=["lbR"])
    rs_ = LB[:, 20:24].rearrange("p (d h) -> p d h", d=2)
    w0 = LB[:, 24:28].rearrange("p (d h) -> p d h", d=2)
    w1 = LB[:, 28:32].rearrange("p (d h) -> p d h", d=2)
    P.op("dve", f_tt(w0, E[:, :, 0, :], rs_, ALU.mult), reads=["lbE", "lbR"], writes=["lbw0"])
    P.op("dve", f_tt(w1, E[:, :, 1, :], rs_, ALU.mult), reads=["lbE", "lbR"], writes=["lbw1"])
    lbv = LB[:, 0:4].rearrange("p (d h) -> p d h", d=2)
    if jlayer == 0:
        P.op("dve", f_tt(lbv, w0, w0, ALU.subtract), reads=["lbw0"], writes=["lb"])
    else:
        cum1 = LB[:, 16:20].rearrange("p (d h) -> p d h", d=2)
        P.op("dve", f_tt(cum1, w0, w1, ALU.add), reads=["lbw0", "lbw1", "lbR"], writes=["lbS"])
        P.op("dve", f_tt(lbv, cum1, w0, ALU.subtract), reads=["lbS", "lbw0"], writes=["lb"])
    P.op("dve", f_ts(LB[:, 4:8], LB[:, 0:4], -1.0, 1.0, ALU.mult, ALU.add), reads=["lb"], writes=["oml"])

    wkc = [0]

    def wk():
        i = wkc[0] % NW
        wkc[0] += 1
        return WK[i], ("WK", i)

    wbc = [0]

    def wbf():
        i = wbc[0] % 8
        wbc[0] += 1
        return WB[i], ("WB", i)

    for grp in range(HG_NGRP):
        t0 = grp * 256
        hb = grp % 2
        P.dma("pool", f_dma(hTg[hb][:], hT_d.ap()[:, t0:t0 + 256].rearrange("(kc p) n -> p kc n", p=128)),
              writes=[("hTg", hb)])
        for pi in range(3):
            mms = []
            for hh in range(2):
                c0 = pi * 256 + hh * 128
                for kt in range(16):
                    mms.append(f_mm(kb.ps[pi][:, hh * 256:(hh + 1) * 256], wsb[:, kt, c0:c0 + 128], hTg[hb][:, kt, :],
                                    kt == 0, kt == 15))
            P.op("pe", f_group(mms), reads=["wsb", ("hTg", hb)], writes=[("ps", pi)])
        for tb in range(2):
            mms = [f_mm(kb.ps[3 + tb][:, :], hTg[hb][:, kt, tb * 128:(tb + 1) * 128], wsb[:, kt, 768:1280], kt == 0, kt == 15)
                   for kt in range(16)]
            P.op("pe", f_group(mms), reads=["wsb", ("hTg", hb)], writes=[("ps", 3 + tb)])
            vb, vbk = wbf()
            P.op("act", f_act(vb[:], kb.ps[3 + tb][:, 0:256], AF.Copy), reads=[("ps", 3 + tb)], writes=[vbk])
            P.dma("sp", f_dma(VVd.ap()[t0 + tb * 128:t0 + (tb + 1) * 128, :], vb[:]), reads=[vbk], writes=[("VV", 2 * grp + tb)])
            sg, sgk = wk()
            P.op("act", f_act(sg[:], kb.ps[3 + tb][:, 256:512], AF.Silu), reads=[("ps", 3 + tb)], writes=[sgk])
            P.dma("sp", f_dma(SGd.ap()[t0 + tb * 128:t0 + (tb + 1) * 128, :], sg[:]), reads=[sgk], writes=[("SG", 2 * grp + tb)])
        for hh in range(2):
            qs, qsk = wk()
            P.op("act", f_act(qs[:], kb.ps[0][:, hh * 256:(hh + 1) * 256], AF.Silu), reads=[("ps", 0)], writes=[qsk])
            for d in range(2):
                zps = kb.ps[1 + d][:, hh * 256:(hh + 1) * 256]
                lbc = LB[:, 2 * d + hh:2 * d + hh + 1]
                omlc = LB[:, 4 + 2 * d + hh:4 + 2 * d + hh + 1]
                f_, fk = wk()
                P.op("act", f_act(f_[:], zps, AF.Sigmoid), reads=[("ps", 1 + d)], writes=[fk])
                P.op("dve", f_ts(f_[:], f_[:], omlc, lbc, ALU.mult, ALU.add), reads=[fk, "lb", "oml"], writes=[fk])
                kk, kkk = wk()
                P.op("pool", f_ts(kk[:], f_[:], -1.0, 1.0, ALU.mult, ALU.add), reads=[fk], writes=[kkk])
                lf, lfk = wk()
                P.op("act", f_act(lf[:], f_[:], AF.Ln), reads=[fk], writes=[lfk])
                cum, cumk = wk()
                P.op("dve", f_scan(cum[:], smask, lf[:], 0.0, ALU.mult, ALU.add), reads=[lfk, "PR"], writes=[cumk])
                totb = bass.AP(cum, 31, [[256, 128], [32, 8], [0, 32]])
                cum3 = cum[:].rearrange("p (c t) -> p c t", c=8)
                e1, e1k = wk()
                e3, e3k = wk()
                if d == 0:
                    e1, e1k = cum, cumk
                    P.op("dve", f_tt(e3[:].rearrange("p (c t) -> p c t", c=8), totb, cum3, ALU.subtract), reads=[cumk],
                         writes=[e3k])
                else:
                    P.op("dve", f_tt(e3[:], cum[:], lf[:], ALU.subtract), reads=[cumk, lfk], writes=[e3k])
                    P.op("dve", f_tt(e1[:].rearrange("p (c t) -> p c t", c=8), totb, e3[:].rearrange("p (c t) -> p c t", c=8),
                                     ALU.subtract), reads=[cumk, e3k], writes=[e1k])
                ci0 = grp * 8
                P.op("act", f_act(FF[:, d, hh, ci0:ci0 + 8], bass.AP(cum, 31, [[256, 128], [32, 8]]), AF.Exp), reads=[cumk],
                     writes=[("FF", d, hh, grp)])
                E1, E1k = wk()
                P.op("act", f_act(E1[:], e1[:], AF.Exp), reads=[e1k], writes=[E1k])
                E2, E2k = wk()
                P.op("act", f_act(E2[:], e1[:], AF.Exp, scale=-1.0), reads=[e1k], writes=[E2k])
                E3, E3k = wk()
                P.op("act", f_act(E3[:], e3[:], AF.Exp), reads=[e3k], writes=[E3k])
                qb, qbk = wbf()
                P.op("dve", f_tt(qb[:], qs[:], E1[:], ALU.mult), reads=[qsk, E1k], writes=[qbk])
                P.dma("sp", f_dma(QBd[d].ap()[hh, :, t0:t0 + 256], qb[:]), reads=[qbk], writes=[("QB", d, hh, grp)])
                kbt, kbk = wbf()
                P.op("pool", f_tt(kbt[:], kk[:], E2[:], ALU.mult), reads=[kkk, E2k], writes=[kbk])
                P.dma("sp", f_dma(KBd[d].ap()[hh, :, t0:t0 + 256], kbt[:]), reads=[kbk], writes=[("KB", d, hh, grp)])
                ke, kek = wk()
                P.op("pool", f_tt(ke[:], kk[:], E3[:], ALU.mult), reads=[kkk, E3k], writes=[kek])
                pb = 5 + (2 * hh + d) % 2
                trs = [f_tr(kb.ps[pb][:, tb * 128:(tb + 1) * 128], ke[:, tb * 128:(tb + 1) * 128], ident) for tb in range(2)]
                P.op("pe", f_group(trs), reads=[kek, "PR"], writes=[("ps", pb)])
                ket, ketk = wbf()
                P.op("act", f_act(ket[:], kb.ps[pb][:, 0:256], AF.Copy), reads=[("ps", pb)], writes=[ketk])
                for tb in range(2):
                    P.dma("sp", f_dma(KEd[d].ap()[t0 + tb * 128:t0 + (tb + 1) * 128, hh * 128:(hh + 1) * 128],
                                      ket[:, tb * 128:(tb + 1) * 128]), reads=[ketk], writes=[("KE", d, hh, 2 * grp + tb)])

    S = kb.sb("Sst", [128, 2, 128])
    Sb = [kb.sb("Sb%d" % i, [128, 2, 128], BF16) for i in range(8)]
    QM = [kb.sb("QM%d" % i, [128, 2, 128], BF16) for i in range(8)]
    SC = [kb.sb("SC%d" % i, [128, 2, 128], BF16) for i in range(2)]
    QBb = [kb.sb("QBb%d" % i, [128, 2, 128], BF16) for i in range(2)]
    KBb = [kb.sb("KBb%d" % i, [128, 2, 128], BF16) for i in range(2)]
    KEb = [kb.sb("KEb%d" % i, [128, 256], BF16) for i in range(2)]
    VVb = [kb.sb("VVb%d" % i, [128, 256], BF16) for i in range(2)]
    OB = [kb.sb("OB%d" % i, [128, 256]) for i in range(2)]
    OFb = [kb.sb("OFb%d" % i, [128, 256]) for i in range(2)]
    SGb = [kb.sb("SGb%d" % i, [128, 256]) for i in range(2)]
    OG = [kb.sb("OG%d" % i, [128, 256]) for i in range(2)]
    KEm = [kb.sb("KEm%d" % i, [128, 256], BF16) for i in range(8)]
    onb = PR[:, HG_OFF_ON:HG_OFF_ON + 256]
    for i in range(8):
        P.op("pool", f_memset(QM[i][:], 0.0), writes=[("QM", i)])
    sbc = [0]
    for d in range(2):
        order = list(range(HG_NBLK)) if d == 0 else [1, 0] + list(range(HG_NBLK - 1, 1, -1))
        mask = PR[:, HG_OFF_MF:HG_OFF_MF + 128] if d == 0 else PR[:, HG_OFF_MR:HG_OFF_MR + 128]
        P.op("dve", f_memset(S[:], 0.0), writes=[("S", 0), ("S", 1)])
        for bi_, blk in enumerate(order):
            tb0 = blk * 128
            grp = blk // 2
            lb = bi_ % 2
            for hh in range(2):
                P.dma("sp", f_dma(QBb[lb][:, hh, :], QBd[d].ap()[hh, :, tb0:tb0 + 128]), reads=[("QB", d, hh, grp)],
                      writes=[("QBb", lb, hh)])
                P.dma("sp", f_dma(KBb[lb][:, hh, :], KBd[d].ap()[hh, :, tb0:tb0 + 128]), reads=[("KB", d, hh, grp)],
                      writes=[("KBb", lb, hh)])
            P.dma("sp", f_dma(KEb[lb][:], KEd[d].ap()[tb0:tb0 + 128, :]), reads=[("KE", d, 0, blk), ("KE", d, 1, blk)],
                  writes=[("KEb", lb)])
            P.dma("sp", f_dma(VVb[lb][:], VVd.ap()[tb0:tb0 + 128, :]), reads=[("VV", blk)], writes=[("VVb", lb)])
            if d == 1:
                P.dma("sp", f_dma(OFb[lb][:], OFd.ap()[tb0:tb0 + 128, :]), reads=[("OF", blk)], writes=[("OFb", lb)])
                P.dma("sp", f_dma(SGb[lb][:], SGd.ap()[tb0:tb0 + 128, :]), reads=[("SG", blk)], writes=[("SGb", lb)])
            corder = [0, 1, 2, 3] if d == 0 else [3, 2, 1, 0]
            qmset = (bi_ % 2) * 4
            for ci in range(4):
                P.op('pool', f_ts(KEm[qmset + ci][:], KEb[lb][:], PR[:, HG_OFF_RM + ci:HG_OFF_RM + ci + 1]),
                     reads=[('KEb', lb), 'PR'], writes=[('KEm', qmset + ci)])
            for hh in range(2):
                pb = hh
                P.op("pe", f_mm(kb.ps[pb][:, 0:128], KBb[lb][:, hh, :], QBb[lb][:, hh, :]),
                     reads=[("KBb", lb, hh), ("QBb", lb, hh)], writes=[("ps", pb)])
                P.op("dve", f_tt(SC[lb][:, hh, :], kb.ps[pb][:, 0:128], mask, ALU.mult), reads=[("ps", pb), "PR"],
                     writes=[("SC", lb, hh)])
                for ci in range(4):
                    P.op("pool", f_copy(QM[qmset + ci][:, hh, 32 * ci:32 * ci + 32], QBb[lb][:, hh, 32 * ci:32 * ci + 32]),
                         reads=[("QBb", lb, hh)], writes=[("QM", qmset + ci, hh)])
            sbs = {}
            for ci in corder:
                chunk = blk * 4 + ci
                si = sbc[0] % 8
                sbc[0] += 1
                sbs[ci] = si
                for hh in range(2):
                    P.op("act", f_act(Sb[si][:, hh, :], S[:, hh, :], AF.Copy), reads=[("S", hh)], writes=[("Sb", si, hh)])
                    pu = 2 + (2 * ci + hh) % 4
                    r0 = 32 * ci
                    P.op("pe", f_mm(kb.ps[pu][:, 0:128], KEm[qmset + ci][:, hh * 128:(hh + 1) * 128],
                                    VVb[lb][:, hh * 128:(hh + 1) * 128]),
                         reads=[("KEm", qmset + ci), ("VVb", lb)], writes=[("ps", pu)])
                    P.op("dve", f_stt(S[:, hh, :], S[:, hh, :], FF[:, d, hh, chunk:chunk + 1], kb.ps[pu][:, 0:128], ALU.mult, ALU.add),
                         reads=[("S", hh), ("ps", pu), ("FF", d, hh, grp)], writes=[("S", hh)])
            for hh in range(2):
                po = 6 + hh
                mms = [f_mm(kb.ps[po][:, 0:128], SC[lb][:, hh, :], VVb[lb][:, hh * 128:(hh + 1) * 128], True, False)]
                for n_, ci in enumerate(corder):
                    mms.append(f_mm(kb.ps[po][:, 0:128], QM[qmset + ci][:, hh, :], Sb[sbs[ci]][:, hh, :], False, n_ == 3))
                P.op("pe", f_group(mms), reads=[("SC", lb, hh), ("VVb", lb)] + [("QM", qmset + ci, hh) for ci in range(4)] +
                     [("Sb", sbs[ci], hh) for ci in range(4)], writes=[("ps", po)])
                if d == 0:
                    P.op("act", f_act(OB[lb][:, hh * 128:(hh + 1) * 128], kb.ps[po][:, 0:128], AF.Copy), reads=[("ps", po)],
                         writes=[("OB", lb, hh)])
                else:
                    P.op("dve", f_tt(OB[lb][:, hh * 128:(hh + 1) * 128], kb.ps[po][:, 0:128], OFb[lb][:, hh * 128:(hh + 1) * 128], ALU.add),
                         reads=[("ps", po), ("OFb", lb)], writes=[("OB", lb, hh)])
                    ob = OB[lb][:, hh * 128:(hh + 1) * 128]
                    og = OG[lb][:, hh * 128:(hh + 1) * 128]
                    (ssq, k0), (ms, k1), (sd, k2), (rs, k3) = stat.next()
                    P.op("act", f_act(og, ob, AF.Square, accum_out=ssq), reads=[("OB", lb, hh)], writes=[("OG", lb, hh), k0])
                    P.op("dve", f_ts(ms, ssq, 1.0 / 128, EPS, ALU.mult, ALU.add), reads=[k0], writes=[k1])
                    P.op("act", f_act(sd, ms, AF.Sqrt), reads=[k1], writes=[k2])
                    P.op("dve", f_recip(rs, sd), reads=[k2], writes=[k3])
                    P.op("dve", f_stt(og, ob, rs, onb[:, hh * 128:(hh + 1) * 128], ALU.mult, ALU.mult),
                         reads=[("OB", lb, hh), k3, "PR"], writes=[("OG", lb, hh)])
                    P.op("pool", f_tt(og, og, SGb[lb][:, hh * 128:(hh + 1) * 128], ALU.mult), reads=[("OG", lb, hh), ("SGb", lb)],
                         writes=[("OG", lb, hh)])
            if d == 0:
                P.dma("sp", f_dma(OFd.ap()[tb0:tb0 + 128, :], OB[lb][:]), reads=[("OB", lb, 0), ("OB", lb, 1)], writes=[("OF", blk)])
            else:
                P.dma("sp", f_dma(og_d.ap()[tb0:tb0 + 128, :], OG[lb][:]), reads=[("OG", lb, 0), ("OG", lb, 1)], out=True)
    return kb.finish()


def hg_prm(inp, j, core):
    prm = np.zeros((128, HG_NP), np.float32)
    fs = slice(256 * core, 256 * core + 256)
    raw = inp["hg_lb_raw"][:, :, fs].reshape(2, 2, 2, 128)
    prm[:, 0:8] = raw.transpose(3, 0, 1, 2).reshape(128, 8)
    prm[:, HG_OFF_ON:HG_OFF_ON + 256] = inp["hg_o_norm"][j, fs][None, :]
    t = np.arange(128)
    same = (t[:, None] // 32) == (t[None, :] // 32)
    prm[:, HG_OFF_MF:HG_OFF_MF + 128] = same & (t[:, None] <= t[None, :])
    prm[:, HG_OFF_MR:HG_OFF_MR + 128] = same & (t[:, None] >= t[None, :])
    prm[:, HG_OFF_SM:HG_OFF_SM + 256] = (np.arange(256) % 32 != 0)[None, :]
    prm[:, HG_OFF_ID:HG_OFF_ID + 128] = np.eye(128)
    for ci in range(4):
        prm[32 * ci:32 * ci + 32, HG_OFF_RM + ci] = 1.0
    return prm


def hg_inputs(h_seq, inp, j):
    hT = np.ascontiguousarray(h_seq.T)
    maps = []
    w = inp["hg_w_in"][j]
    for core in range(NC):
        fs = slice(256 * core, 256 * core + 256)
        cols = [w[:, 0 * D:1 * D][:, fs], w[:, 2 * D:3 * D][:, fs], w[:, 3 * D:4 * D][:, fs], w[:, 1 * D:2 * D][:, fs],
                w[:, 4 * D:5 * D][:, fs]]
        maps.append({"hT": hT, "w5": np.ascontiguousarray(np.concatenate(cols, axis=1)), "prm": hg_prm(inp, j, core)})
    return maps


_PROGS = {}


def _prog(key, fn):
    if key not in _PROGS:
        _PROGS[key] = fn()
    return _PROGS[key]


def _run(nc, maps):
    res = run_bass_kernel_spmd(nc, maps, core_ids=list(range(NC)))
    return res.results


def _shard_tok(lat, ctx):
    return [np.ascontiguousarray(np.concatenate([lat[1024 * i:1024 * (i + 1)], ctx[32 * i:32 * (i + 1)]])) for i in range(NC)]


def _unshard_tok(parts):
    lat = np.concatenate([p[:1024] for p in parts])
    ctx = np.concatenate([p[1024:] for p in parts])
    return lat, ctx


def _to_colmajor(a):
    return a.reshape(128, 64, -1).transpose(1, 0, 2).reshape(SEQ, -1)


def _to_raster(a):
    return a.reshape(64, 128, -1).transpose(1, 0, 2).reshape(SEQ, -1)


def kernel(x, c, ctx, c_ctx, w_mod, b_mod, norm_mix_pre, norm_mix_post, norm_mlp_pre, norm_mlp_post,
           w_mlp_in, w_mlp_out, s5_a_re, s5_a_im, s5_log_dt, s5_b_re, s5_b_im, s5_c_re, s5_c_im, s5_d,
           s5_w_glu, hg_w_in, hg_lb_raw, hg_o_norm, hg_w_out):
    f = lambda a: np.ascontiguousarray(np.asarray(a, dtype=np.float32))
    inp = {"s5_a_re": f(s5_a_re), "s5_a_im": f(s5_a_im), "s5_log_dt": f(s5_log_dt), "s5_b_re": f(s5_b_re),
           "s5_b_im": f(s5_b_im), "s5_c_re": f(s5_c_re), "s5_c_im": f(s5_c_im), "s5_d": f(s5_d),
           "hg_w_in": f(hg_w_in), "hg_lb_raw": f(hg_lb_raw), "hg_o_norm": f(hg_o_norm)}
    w_mod, b_mod = f(w_mod), f(b_mod)
    g_pre1, g_post1, g_pre2, g_post2 = f(norm_mix_pre), f(norm_mix_post), f(norm_mlp_pre), f(norm_mlp_post)
    w_mlp_in, w_mlp_out, s5_w_glu, hg_w_out = f(w_mlp_in), f(w_mlp_out), f(s5_w_glu), f(hg_w_out)
    ident = np.eye(128, dtype=np.float32)
    cond = np.stack([f(c)[0], f(c_ctx)])
    condT = np.ascontiguousarray(cond.reshape(2, 16, 128).transpose(2, 1, 0))
    maps = [{"condT": condT, "wm": np.ascontiguousarray(w_mod[:, :, i * 1536:(i + 1) * 1536]),
             "bm": np.ascontiguousarray(b_mod[:, i * 1536:(i + 1) * 1536])} for i in range(NC)]
    mod = np.concatenate([r["mod"] for r in _run(_prog("M", build_M), maps)], axis=2)
    xs = _shard_tok(f(x)[0], f(ctx)[0])
    maps = [{"x": xs[i], "gain": g_pre1[0], "mod": mod[0]} for i in range(NC)]
    hs = [r["h"] for r in _run(_prog("A", build_A), maps)]
    for layer in range(4):
        j = layer // 2
        last = layer == 3
        h_lat, h_ctx = _unshard_tok(hs)
        if layer % 2 == 0:
            h_seq = np.concatenate([h_ctx, h_lat])
            res = _run(_prog("BS5", build_BS5), s5_inputs(h_seq, inp, j))
            y_seq = s5_gather(res)
            ys = _shard_tok(y_seq[CTXL:], y_seq[:CTXL])
            kind, wmix = "s5", s5_w_glu[j]
        else:
            h_seq = np.concatenate([h_ctx, _to_colmajor(h_lat)])
            res = _run(_prog(("BHG", j), lambda: build_BHG(j)), hg_inputs(h_seq, inp, j))
            og = np.concatenate([r["og"] for r in res], axis=1)
            ys = _shard_tok(_to_raster(og[CTXL:]), og[:CTXL])
            kind, wmix = "hg", hg_w_out[j]
        com = {"wmix": wmix, "w1": w_mlp_in[layer], "w2": w_mlp_out[layer], "mod": mod[layer],
               "g_post1": g_post1[layer], "g_pre2": g_pre2[layer], "g_post2": g_post2[layer], "ident": ident}
        if not last:
            com["modn"] = mod[layer + 1]
            com["g_pren"] = g_pre1[layer + 1]
        maps = [dict(com, x=xs[i], yin=ys[i]) for i in range(NC)]
        res = _run(_prog(("C", kind, last), lambda: build_C(kind, last)), maps)
        xs = [r["x2"] for r in res]
        if not last:
            hs = [r["hn"] for r in res]
    lat, _ = _unshard_tok(xs)
    return lat.reshape(1, SEQ, D).astype(np.float32)
```
